# Optimizing a Trainium2 kernel written in Bass

```python
import math
import jax
import jax.numpy as jnp
from jax import lax
import numpy as np

D_MODEL = 4096
BATCH = 2
SEQ = 8192
DEPTH = 4

CTX_LEN = 256
GRID_W = 64
N_BRANCH = 4
MIX_W = D_MODEL // N_BRANCH
HEAD_DIM = 128
ROPE_BASE = 10000.0
Q_BLOCK = 128
NEG_INF = -1e30
EPS = 1e-6

WA_HEADS = MIX_W // HEAD_DIM
WA_KV_HEADS = 2
WA_GROUP = WA_HEADS // WA_KV_HEADS
WINDOW = 128

NA_HEADS = MIX_W // HEAD_DIM
NA_KH = 8
NA_KW = 16

MLA_HEADS = MIX_W // 128
MLA_Q_LORA = 896
MLA_KV_LORA = 512
MLA_NOPE = 128
MLA_ROPE = 64
MLA_V = 128

DIFF_DIM = 64
DIFF_HEADS = MIX_W // (2 * DIFF_DIM)
DIFF_V = 2 * DIFF_DIM

IN_SIZES = (
    WA_HEADS * HEAD_DIM, WA_KV_HEADS * HEAD_DIM, WA_KV_HEADS * HEAD_DIM, MIX_W,
    MIX_W, MIX_W, MIX_W, MIX_W,
    MLA_Q_LORA, MLA_KV_LORA, MLA_ROPE, MIX_W,
    DIFF_HEADS * 2 * DIFF_DIM, DIFF_HEADS * 2 * DIFF_DIM, DIFF_HEADS * DIFF_V, MIX_W,
    N_BRANCH * D_MODEL,
)
N_IN = sum(IN_SIZES)

kernel_name = "hybrid_parallel_branch_diffusion_trunk"


def rms_norm(x, g):
    xf = x.astype(jnp.float32)
    y = xf * lax.rsqrt(jnp.mean(xf * xf, axis=-1, keepdims=True) + EPS)
    return (y * g).astype(x.dtype)


def heads(a, *dims):
    return a.reshape(a.shape[:2] + dims)


def split_columns(p):
    points = [int(v) for v in np.cumsum(IN_SIZES)[:-1]]
    return jnp.split(p, points, axis=-1)


def axial_rope_tables(n_tok, d_rot):
    t = jnp.arange(n_tok, dtype=jnp.int32)
    row = (t // GRID_W).astype(jnp.float32)
    col = (t % GRID_W).astype(jnp.float32)
    n_f = d_rot // 4
    inv = jnp.power(ROPE_BASE, -jnp.arange(n_f, dtype=jnp.float32) / n_f)
    ang = jnp.concatenate([row[:, None] * inv, col[:, None] * inv], axis=-1)
    return jnp.cos(ang), jnp.sin(ang)


def apply_rope(x, cos, sin):
    d2 = x.shape[-1] // 2
    shape = (x.shape[1],) + (1,) * (x.ndim - 3) + (d2,)
    cs = cos.reshape(shape).astype(x.dtype)
    sn = sin.reshape(shape).astype(x.dtype)
    x1, x2 = x[..., :d2], x[..., d2:]
    return jnp.concatenate([x1 * cs - x2 * sn, x2 * cs + x1 * sn], axis=-1)


def softmax_f32(s):
    return jax.nn.softmax(s.astype(jnp.float32), axis=-1)


def sweep_query_blocks(f, *qs):
    B, S = qs[0].shape[:2]
    nb = S // Q_BLOCK
    blocks = tuple(jnp.moveaxis(a.reshape((B, nb, Q_BLOCK) + a.shape[2:]), 1, 0) for a in qs)
    out = lax.map(lambda args: f(*args), blocks)
    out = jnp.moveaxis(out, 0, 1)
    return out.reshape((B, S) + out.shape[3:])


def window_gqa(q, k, v, qc, kc, vc, qn, kn, sink, cos, sin, ctx_out):
    B, S = q.shape[:2]
    C = kc.shape[1]
    nb = S // Q_BLOCK
    G, R, d = WA_KV_HEADS, WA_GROUP, HEAD_DIM
    scale = d ** -0.5
    q = apply_rope(rms_norm(q, qn), cos, sin)
    k = apply_rope(rms_norm(k, kn), cos, sin)
    kc = rms_norm(kc, kn)
    qb = q.reshape(B, nb, Q_BLOCK, G, R, d)

    def band(a):
        ap = jnp.pad(a, ((0, 0), (Q_BLOCK, Q_BLOCK), (0, 0), (0, 0))).reshape(B, nb + 2, Q_BLOCK, G, d)
        return jnp.concatenate([ap[:, :-2], ap[:, 1:-1], ap[:, 2:]], axis=2)

    kb, vb = band(k), band(v)
    q_pos = jnp.arange(S).reshape(nb, Q_BLOCK, 1)
    k_pos = (jnp.arange(nb) * Q_BLOCK - Q_BLOCK)[:, None, None] + jnp.arange(3 * Q_BLOCK)[None, None, :]
    valid = (k_pos >= 0) & (k_pos < S) & (jnp.abs(q_pos - k_pos) <= WINDOW)
    s_lat = jnp.einsum('bnqgrd,bnkgd->bgrnqk', qb, kb).astype(jnp.float32) * scale
    s_lat = jnp.where(valid, s_lat, NEG_INF)
    s_ctx = jnp.einsum('bnqgrd,bcgd->bgrnqc', qb, kc).astype(jnp.float32) * scale
    sink_col = jnp.broadcast_to(sink.astype(jnp.float32).reshape(1, G, R, 1, 1, 1), s_ctx.shape[:-1] + (1,))
    p = softmax_f32(jnp.concatenate([s_lat, s_ctx, sink_col], axis=-1))
    nk = 3 * Q_BLOCK
    o = (jnp.einsum('bgrnqk,bnkgd->bnqgrd', p[..., :nk].astype(v.dtype), vb)
         + jnp.einsum('bgrnqc,bcgd->bnqgrd', p[..., nk:nk + C].astype(v.dtype), vc))
    o = o.reshape(B, S, WA_HEADS * d)
    oc = None
    if ctx_out:
        qcn = rms_norm(qc, qn).reshape(B, C, G, R, d)
        s = jnp.einsum('bqgrd,bkgd->bgrqk', qcn, kc).astype(jnp.float32) * scale
        sc = jnp.broadcast_to(sink.astype(jnp.float32).reshape(1, G, R, 1, 1), s.shape[:-1] + (1,))
        pc = softmax_f32(jnp.concatenate([s, sc], axis=-1))
        oc = jnp.einsum('bgrqk,bkgd->bqgrd', pc[..., :C].astype(vc.dtype), vc).reshape(B, C, WA_HEADS * d)
    return o, oc


def neighbourhood_attn(q, k, v, qc, kc, vc, qn, kn, rpb, rows, ctx_out):
    B, S = q.shape[:2]
    C = kc.shape[1]
    H, d, W = NA_HEADS, HEAD_DIM, GRID_W
    kh = min(NA_KH, rows)
    scale = d ** -0.5
    q = rms_norm(q, qn)
    k = rms_norm(k, kn)
    kc = rms_norm(kc, kn)
    qg = q.reshape(B, rows, W, H, d)
    kg = k.reshape(B, rows, W, H, d)
    vg = v.reshape(B, rows, W, H, d)
    r = jnp.arange(rows)
    row_start = jnp.clip(r - kh // 2, 0, rows - kh)
    key_rows = row_start[:, None] + jnp.arange(kh)[None, :]
    k_rows = kg[:, key_rows]
    v_rows = vg[:, key_rows]
    w = jnp.arange(W)
    col_start = jnp.clip(w - NA_KW // 2, 0, W - NA_KW)
    col_ok = (w[None, :] >= col_start[:, None]) & (w[None, :] < col_start[:, None] + NA_KW)
    dr = key_rows - r[:, None] + (NA_KH - 1)
    dc = jnp.clip(w[None, :] - w[:, None], -(NA_KW - 1), NA_KW - 1) + (NA_KW - 1)
    bias = rpb[:, dr[:, None, :, None], dc[None, :, None, :]]
    s_lat = jnp.einsum('brqhd,brkwhd->bhrqkw', qg, k_rows).astype(jnp.float32) * scale + bias.astype(jnp.float32)
    s_lat = jnp.where(col_ok[:, None, :], s_lat, NEG_INF).reshape(B, H, rows, W, kh * W)
    s_ctx = jnp.einsum('brqhd,bchd->bhrqc', qg, kc).astype(jnp.float32) * scale
    p = softmax_f32(jnp.concatenate([s_lat, s_ctx], axis=-1))
    p_lat = p[..., :kh * W].reshape(B, H, rows, W, kh, W).astype(v.dtype)
    o = (jnp.einsum('bhrqkw,brkwhd->brqhd', p_lat, v_rows)
         + jnp.einsum('bhrqc,bchd->brqhd', p[..., kh * W:].astype(v.dtype), vc))
    o = o.reshape(B, S, H * d)
    oc = None
    if ctx_out:
        qcn = rms_norm(qc, qn)
        pc = softmax_f32(jnp.einsum('bqhd,bkhd->bhqk', qcn, kc) * scale)
        oc = jnp.einsum('bhqk,bkhd->bqhd', pc.astype(vc.dtype), vc).reshape(B, C, H * d)
    return o, oc


def latent_attn(cq, ckv, kpe, cq_c, ckv_c, kpe_c, g_qa, g_kva, w_q_up, w_kv_up,
                qn_nope, qn_pe, kn_nope, kn_pe, cos, sin, ctx_out):
    B, S = cq.shape[:2]
    C = cq_c.shape[1]
    H = MLA_HEADS
    scale = (MLA_NOPE + MLA_ROPE) ** -0.5

    def queries(cqx):
        qh = heads(rms_norm(cqx, g_qa) @ w_q_up, H, MLA_NOPE + MLA_ROPE)
        return rms_norm(qh[..., :MLA_NOPE], qn_nope), rms_norm(qh[..., MLA_NOPE:], qn_pe)

    def keys_values(ckvx, kpex):
        kv = heads(rms_norm(ckvx, g_kva) @ w_kv_up, H, MLA_NOPE + MLA_V)
        return rms_norm(kv[..., :MLA_NOPE], kn_nope), rms_norm(kpex, kn_pe), kv[..., MLA_NOPE:]

    def attend(qn_, qp_, kn_, kp_, vv):
        s = (jnp.einsum('bqhd,bkhd->bhqk', qn_, kn_) + jnp.einsum('bqhd,bkd->bhqk', qp_, kp_)).astype(jnp.float32) * scale
        p = softmax_f32(s).astype(vv.dtype)
        return jnp.einsum('bhqk,bkhd->bqhd', p, vv)

    q_nope, q_pe = queries(cq)
    q_pe = apply_rope(q_pe, cos, sin)
    k_nope, k_pe, v = keys_values(ckv, kpe)
    k_pe = apply_rope(k_pe, cos, sin)
    kn_c, kpe_cn, v_c = keys_values(ckv_c, kpe_c)
    K_nope = jnp.concatenate([k_nope, kn_c], axis=1)
    K_pe = jnp.concatenate([k_pe, kpe_cn], axis=1)
    V = jnp.concatenate([v, v_c], axis=1)
    o = sweep_query_blocks(lambda a, b_: attend(a, b_, K_nope, K_pe, V), q_nope, q_pe).reshape(B, S, H * MLA_V)
    oc = None
    if ctx_out:
        qc_n, qc_p = queries(cq_c)
        oc = attend(qc_n, qc_p, kn_c, kpe_cn, v_c).reshape(B, C, H * MLA_V)
    return o, oc


def diff_attn(q, k, v, qc, kc, vc, qn, kn, lam_q1, lam_k1, lam_q2, lam_k2, subln, lambda_init, cos, sin, ctx_out):
    B, S = q.shape[:2]
    C = kc.shape[1]
    scale = DIFF_DIM ** -0.5
    lam = (jnp.exp(jnp.sum(lam_q1.astype(jnp.float32) * lam_k1.astype(jnp.float32)))
           - jnp.exp(jnp.sum(lam_q2.astype(jnp.float32) * lam_k2.astype(jnp.float32))) + lambda_init)
    q = apply_rope(rms_norm(q, qn), cos, sin)
    k = apply_rope(rms_norm(k, kn), cos, sin)
    kc = rms_norm(kc, kn)

    def attend(qq, kk, vv):
        s = jnp.einsum('bqhid,bkhid->bihqk', qq, kk).astype(jnp.float32) * scale
        p = softmax_f32(s)
        a = (p[:, 0] - lam * p[:, 1]).astype(vv.dtype)
        o = jnp.einsum('bhqk,bkhd->bqhd', a, vv)
        return rms_norm(o, subln) * (1.0 - lambda_init)

    K = jnp.concatenate([k, kc], axis=1)
    V = jnp.concatenate([v, vc], axis=1)
    o = sweep_query_blocks(lambda a: attend(a, K, V), q).reshape(B, S, DIFF_HEADS * DIFF_V)
    oc = None
    if ctx_out:
        oc = attend(rms_norm(qc, qn), kc, vc).reshape(B, C, DIFF_HEADS * DIFF_V)
    return o, oc


def merge_branches(outs, gate_paths, merge_gate, w_branch, w_out):
    mg = merge_gate.reshape(merge_gate.shape[:2] + (N_BRANCH, D_MODEL))
    merged = sum(jax.nn.sigmoid(mg[:, :, i]) * ((outs[i] * jax.nn.silu(gate_paths[i])) @ w_branch[i])
                 for i in range(N_BRANCH))
    return merged @ w_out


def setup_inputs(seed: int = 0) -> dict:
    key = jax.random.key(seed)
    ks = iter(jax.random.split(key, 40))
    L, D = DEPTH, D_MODEL

    def nrm(shape, scale):
        return jax.random.normal(next(ks), shape, jnp.float32) * scale

    def gain(shape):
        return 1.0 + nrm(shape, 0.01)

    return {
        "x": nrm((BATCH, SEQ, D), 1.0),
        "c": nrm((BATCH, D), 1.0),
        "ctx": nrm((BATCH, CTX_LEN, D), 1.0),
        "c_ctx": nrm((D,), 1.0),
        "w_ada": nrm((L, D, 3 * D), 0.5 * D ** -0.5),
        "b_ada": nrm((L, 3 * D), 0.02),
        "g_norm": gain((L, D)),
        "w_in": nrm((L, D, N_IN), D ** -0.5),
        "qn_a": gain((L, HEAD_DIM)),
        "kn_a": gain((L, HEAD_DIM)),
        "sink_a": nrm((L, WA_HEADS), 0.5),
        "qn_b": gain((L, HEAD_DIM)),
        "kn_b": gain((L, HEAD_DIM)),
        "rpb_b": nrm((L, NA_HEADS, 2 * NA_KH - 1, 2 * NA_KW - 1), 0.1),
        "g_qa": gain((L, MLA_Q_LORA)),
        "g_kva": gain((L, MLA_KV_LORA)),
        "w_q_up": nrm((L, MLA_Q_LORA, MLA_HEADS * (MLA_NOPE + MLA_ROPE)), MLA_Q_LORA ** -0.5),
        "w_kv_up": nrm((L, MLA_KV_LORA, MLA_HEADS * (MLA_NOPE + MLA_V)), MLA_KV_LORA ** -0.5),
        "qn_nope": gain((L, MLA_NOPE)),
        "qn_pe": gain((L, MLA_ROPE)),
        "kn_nope": gain((L, MLA_NOPE)),
        "kn_pe": gain((L, MLA_ROPE)),
        "qn_d": gain((L, DIFF_DIM)),
        "kn_d": gain((L, DIFF_DIM)),
        "lam_q1": nrm((L, DIFF_DIM), 0.1),
        "lam_k1": nrm((L, DIFF_DIM), 0.1),
        "lam_q2": nrm((L, DIFF_DIM), 0.1),
        "lam_k2": nrm((L, DIFF_DIM), 0.1),
        "subln_d": gain((L, DIFF_V)),
        "w_branch": nrm((L, N_BRANCH, MIX_W, D), MIX_W ** -0.5),
        "w_out": nrm((L, D, D), D ** -0.5),
    }


def reference(x, c, ctx, c_ctx, w_ada, b_ada, g_norm, w_in, qn_a, kn_a, sink_a,
              qn_b, kn_b, rpb_b, g_qa, g_kva, w_q_up, w_kv_up, qn_nope, qn_pe, kn_nope, kn_pe,
              qn_d, kn_d, lam_q1, lam_k1, lam_q2, lam_k2, subln_d, w_branch, w_out):
    B, S, _ = x.shape
    rows = S // GRID_W
    cos_a, sin_a = axial_rope_tables(S, HEAD_DIM)
    cos_c, sin_c = axial_rope_tables(S, MLA_ROPE)
    cos_d, sin_d = axial_rope_tables(S, DIFF_DIM)
    silu_c = jax.nn.silu(c)
    silu_cc = jax.nn.silu(c_ctx)
    for l in range(DEPTH):
        ctx_out = l < DEPTH - 1
        lambda_init = 0.8 - 0.6 * math.exp(-0.3 * l)
        sh, sc, gt = jnp.split(silu_c @ w_ada[l] + b_ada[l], 3, axis=-1)
        sh_c, sc_c, gt_c = jnp.split(silu_cc @ w_ada[l] + b_ada[l], 3, axis=-1)
        h = rms_norm(x, g_norm[l]) * (1.0 + sc[:, None]) + sh[:, None]
        hc = rms_norm(ctx, g_norm[l]) * (1.0 + sc_c) + sh_c
        (aq, ak, av, ag, bq, bk, bv, bg, cq, ckv, cpe, cg, dq, dk, dv, dg, mg) = split_columns(h @ w_in[l])
        (aq_c, ak_c, av_c, ag_c, bq_c, bk_c, bv_c, bg_c, cq_c, ckv_c, cpe_c, cg_c,
         dq_c, dk_c, dv_c, dg_c, mg_c) = split_columns(hc @ w_in[l])

        ya, ya_c = window_gqa(heads(aq, WA_HEADS, HEAD_DIM), heads(ak, WA_KV_HEADS, HEAD_DIM), heads(av, WA_KV_HEADS, HEAD_DIM),
                              heads(aq_c, WA_HEADS, HEAD_DIM), heads(ak_c, WA_KV_HEADS, HEAD_DIM), heads(av_c, WA_KV_HEADS, HEAD_DIM),
                              qn_a[l], kn_a[l], sink_a[l], cos_a, sin_a, ctx_out)
        yb, yb_c = neighbourhood_attn(heads(bq, NA_HEADS, HEAD_DIM), heads(bk, NA_HEADS, HEAD_DIM), heads(bv, NA_HEADS, HEAD_DIM),
                                      heads(bq_c, NA_HEADS, HEAD_DIM), heads(bk_c, NA_HEADS, HEAD_DIM), heads(bv_c, NA_HEADS, HEAD_DIM),
                                      qn_b[l], kn_b[l], rpb_b[l], rows, ctx_out)
        yc, yc_c = latent_attn(cq, ckv, cpe, cq_c, ckv_c, cpe_c, g_qa[l], g_kva[l], w_q_up[l], w_kv_up[l],
                               qn_nope[l], qn_pe[l], kn_nope[l], kn_pe[l], cos_c, sin_c, ctx_out)
        yd, yd_c = diff_attn(heads(dq, DIFF_HEADS, 2, DIFF_DIM), heads(dk, DIFF_HEADS, 2, DIFF_DIM), heads(dv, DIFF_HEADS, DIFF_V),
                             heads(dq_c, DIFF_HEADS, 2, DIFF_DIM), heads(dk_c, DIFF_HEADS, 2, DIFF_DIM), heads(dv_c, DIFF_HEADS, DIFF_V),
                             qn_d[l], kn_d[l], lam_q1[l], lam_k1[l], lam_q2[l], lam_k2[l], subln_d[l], lambda_init,
                             cos_d, sin_d, ctx_out)

        x = x + gt[:, None] * merge_branches((ya, yb, yc, yd), (ag, bg, cg, dg), mg, w_branch[l], w_out[l])
        if ctx_out:
            ctx = ctx + gt_c * merge_branches((ya_c, yb_c, yc_c, yd_c), (ag_c, bg_c, cg_c, dg_c), mg_c, w_branch[l], w_out[l])
    return x
```

```python
import math
from contextlib import ExitStack

import numpy as np
import ml_dtypes

import concourse.bass as bass
import concourse.mybir as mybir
from concourse.bass_utils import run_bass_kernel_spmd

F32 = mybir.dt.float32
BF16 = mybir.dt.bfloat16
ALU = mybir.AluOpType
AF = mybir.ActivationFunctionType
NPBF = ml_dtypes.bfloat16

D = 4096
KC = D // 128
BATCH = 2
SEQ = 8192
CTX = 256
TB = SEQ + CTX
DEPTH = 4
GRID_W = 64
EPS = 1e-6
NCORES = 8
TSH = SEQ // 4
CSH = CTX // 4
TS = TSH + CSH


class Buf:
    __slots__ = ("name", "w", "r", "dsem", "dcnt", "excl")

    def __init__(self, name, excl=False):
        self.name = name
        self.excl = excl
        self.w = None
        self.r = {}
        self.dsem = None
        self.dcnt = 0


class Sched:
    def __init__(self, nc, stack):
        self.nc = nc
        self.stack = stack
        self.engs = {"pe": nc.tensor, "act": nc.scalar, "dve": nc.vector, "pool": nc.gpsimd, "sp": nc.sync}
        self.sem = {}
        self.cnt = {}
        self.seen = {}
        for e in ("pe", "act", "dve", "pool"):
            self.sem[e] = stack.enter_context(nc.semaphore("sem_" + e))
            self.cnt[e] = 0
        for e in self.engs:
            self.seen[e] = {}
        self.bufs = []
        self.nsem = 0

    def buf(self, name, excl=False):
        b = Buf(name, excl)
        self.bufs.append(b)
        return b

    def _wait(self, e, evs):
        eng = self.engs[e]
        seen = self.seen[e]
        for key, (sem, val) in evs.items():
            if key == "pe" and e == "pe":
                continue
            if seen.get(key, 0) >= val:
                continue
            if key in self.cnt:
                assert val <= self.cnt[key], ("waiting for an inc never issued", key, val, self.cnt[key])
            eng.wait_ge(sem, val)
            seen[key] = val

    @staticmethod
    def _add(evs, ev):
        if ev is None:
            return
        key, sem, val = ev
        if key not in evs or evs[key][1] < val:
            evs[key] = (sem, val)

    def _deps(self, reads, writes, e=None):
        evs = {}
        for b in reads:
            self._add(evs, b.w)
            if b.excl:
                for key, (sem, val) in b.r.items():
                    if key != e:
                        self._add(evs, (key, sem, val))
        for b in writes:
            self._add(evs, b.w)
            for key, (sem, val) in b.r.items():
                self._add(evs, (key, sem, val))
        return evs

    def op(self, e, fn, reads=(), writes=(), inc=True):
        self._wait(e, self._deps(reads, writes, e))
        ins = fn(self.engs[e])
        if inc:
            self.cnt[e] += 1
            ins.then_inc(self.sem[e], 1)
            ev = (e, self.sem[e], self.cnt[e])
        else:
            ev = (e, self.sem[e], self.cnt[e] + 1)
        for b in reads:
            if e not in b.r or b.r[e][1] < ev[2]:
                b.r[e] = (ev[1], ev[2])
        for b in writes:
            b.w = ev
            b.r = {}
        return ins

    def dma(self, out_ap, in_ap, sb, load, q="sp", **kw):
        if sb.dsem is None:
            sb.dsem = self.stack.enter_context(self.nc.semaphore("dsem%d" % self.nsem))
            self.nsem += 1
        if load:
            evs = self._deps((), (sb,))
        else:
            evs = self._deps((sb,), ())
        self._wait(q, evs)
        ins = self.engs[q].dma_start(out=out_ap, in_=in_ap, **kw)
        sb.dcnt += 16
        ins.then_inc(sb.dsem, 16)
        key = "d_" + sb.name
        if load:
            sb.w = (key, sb.dsem, sb.dcnt)
            sb.r = {}
        else:
            sb.r[key] = (sb.dsem, sb.dcnt)
        return ins

    def sync_all(self):
        evs = {}
        for e in ("pe", "act", "dve", "pool"):
            if self.cnt[e] > 0:
                evs[e] = (self.sem[e], self.cnt[e])
        for b in self.bufs:
            if b.dsem is not None and b.dcnt > 0:
                evs["d_" + b.name] = (b.dsem, b.dcnt)
        for e in self.engs:
            eng = self.engs[e]
            seen = self.seen[e]
            for key, (sem, val) in evs.items():
                if seen.get(key, 0) >= val:
                    continue
                eng.wait_ge(sem, val)
                seen[key] = val


class Ctx:
    def __init__(self, name):
        self.nc = bass.Bass("TRN2", target_bir_lowering=False, name=name)
        self.stack = ExitStack()
        self.s = Sched(self.nc, self.stack)
        self.nbuf = 0

    def dram_in(self, name, shape, dt=F32):
        return self.nc.dram_tensor(name, list(shape), dt, kind="ExternalInput").ap()

    def dram_out(self, name, shape, dt=F32):
        return self.nc.dram_tensor(name, list(shape), dt, kind="ExternalOutput").ap()

    def sb(self, name, shape, dt=F32):
        self.nbuf += 1
        name = "sb%d_%s" % (self.nbuf, name)
        t = self.stack.enter_context(self.nc.sbuf_tensor(name, list(shape), dt))
        return t, self.s.buf(name)

    def ps(self, name, shape=(128, 512), dt=F32):
        self.nbuf += 1
        name = "ps%d_%s" % (self.nbuf, name)
        t = self.stack.enter_context(self.nc.psum_tensor(name, list(shape), dt))
        return t, self.s.buf(name, excl=True)

    def close(self):
        self.s.sync_all()
        self.stack.close()
        return self.nc


def _const_tiles(c):
    s = c.s
    ones_f, b_ones_f = c.sb("ones_f", [128, 128], F32)
    eps_t, b_eps = c.sb("eps_t", [128, 1], F32)
    s.op("dve", lambda e: e.memset(ones_f[:], 1.0), (), (b_ones_f,))
    s.op("dve", lambda e: e.memset(eps_t[:], EPS), (), (b_eps,))
    return (ones_f, b_ones_f), (eps_t, b_eps)


MCOLS = 3 * D // NCORES
MCH = MCOLS // 128


def build_M():
    c = Ctx("progM")
    s = c.s
    cT = c.dram_in("cT", [128, KC, 4])
    wad = c.dram_in("wad", [DEPTH, 128, KC, MCOLS])
    bad = c.dram_in("bad", [128, DEPTH * MCH])
    mod = c.dram_out("mod", [128, DEPTH * MCH, 4])
    ct, b_ct = c.sb("ct", [128, KC, 4])
    sct, b_sct = c.sb("sct", [128, KC, 4])
    bt, b_bt = c.sb("bt", [128, DEPTH * MCH])
    ot, b_ot = c.sb("ot", [128, DEPTH * MCH, 4])
    wts = [c.sb("w%d" % i, [128, KC, 512]) for i in range(2)]
    pss = [c.ps("ps%d" % i) for i in range(2)]
    s.dma(ct[:], cT[:, :, :], b_ct, True)
    s.dma(bt[:], bad[:, :], b_bt, True)
    s.op("act", lambda e: e.activation(out=sct[:], in_=ct[:], func=AF.Silu), (b_ct,), (b_sct,))
    it = 0
    for l in range(DEPTH):
        for g in range(MCOLS // 512):
            wt, b_wt = wts[it % 2]
            s.dma(wt[:], wad[l, :, :, g * 512:(g + 1) * 512], b_wt, True)
            for j in range(4):
                pt, b_pt = pss[(it * 4 + j) % 2]
                for kc in range(KC):
                    s.op("pe", lambda e, kc=kc, j=j, wt=wt, pt=pt: e.matmul(
                        pt[:, 0:4], wt[:, kc, j * 128:(j + 1) * 128], sct[:, kc, :],
                        start=(kc == 0), stop=(kc == KC - 1)),
                        (b_wt, b_sct), (b_pt,), inc=(kc == KC - 1))
                ch = l * MCH + g * 4 + j
                s.op("dve", lambda e, ch=ch, pt=pt: e.tensor_scalar(
                    out=ot[:, ch, :], in0=pt[:, 0:4], scalar1=bt[:, ch:ch + 1], scalar2=None, op0=ALU.add),
                    (b_pt, b_bt), (b_ot,))
            it += 1
    s.dma(mod[:, :, :], ot[:], b_ot, False)
    return c.close()


class Rot:
    def __init__(self, c, name, shape, dt, n):
        self.t = [c.sb("%s%d" % (name, i), shape, dt) for i in range(n)]
        self.i = 0

    def get(self):
        r = self.t[self.i % len(self.t)]
        self.i += 1
        return r


class PsRot:
    def __init__(self, c, n, prefix="ps"):
        self.t = [c.ps("%s%d" % (prefix, i)) for i in range(n)]
        self.i = 0

    def get(self):
        r = self.t[self.i % len(self.t)]
        self.i += 1
        return r


class WPool:
    def __init__(self, c, nk, ncols, nst=3, nbf=3, name="w"):
        self.c = c
        self.st = Rot(c, name + "st", [128, nk, ncols], F32, nst)
        self.bf = Rot(c, name + "bf", [128, nk, ncols], BF16, nbf)

    def get(self, dram_ap, nk, ncols, rows=128):
        s = self.c.s
        st, b_st = self.st.get()
        wb, b_wb = self.bf.get()
        s.dma(st[0:rows, 0:nk, 0:ncols], dram_ap, b_st, True)
        s.op("pool", lambda e: e.tensor_copy(out=wb[0:rows, 0:nk, 0:ncols], in_=st[0:rows, 0:nk, 0:ncols]),
             (b_st,), (b_wb,))
        return wb, b_wb


def emit_modprep(c, gn_d, modv_d, tag):
    s = c.s
    gn, b_gn = c.sb(tag + "gn", [128, KC])
    mv, b_mv = c.sb(tag + "mv", [128, 6, KC])
    A, b_A = c.sb(tag + "A", [128, 2, KC])
    tmp, b_tmp = c.sb(tag + "tmpA", [128, KC])
    s.dma(gn[:], gn_d[:, :], b_gn, True)
    s.dma(mv[:], modv_d[:, :, :], b_mv, True)
    for j, src in ((0, 1), (1, 4)):
        s.op("dve", lambda e, src=src: e.tensor_scalar(out=tmp[:], in0=mv[:, src, :], scalar1=1.0, scalar2=None,
                                                       op0=ALU.add), (b_mv,), (b_tmp,))
        s.op("dve", lambda e, j=j: e.tensor_tensor(out=A[:, j, :], in0=tmp[:], in1=gn[:], op=ALU.mult),
             (b_tmp, b_gn), (b_A,))
    return (A, b_A), (mv, b_mv)


def emit_norm(c, K, xt, b_x, n, is_ctx, ht, b_h):
    s = c.s
    (A, b_A), (mv, b_mv) = K["AB"]
    ones_f, b_ones = K["ones"]
    eps_t, b_eps = K["eps"]
    j = 1 if is_ctx else 0
    shi = 3 if is_ctx else 0
    pss, b_pss = K["psrot"].get()
    for kc in range(KC):
        sq, b_sq = K["sq"].get()
        s.op("act", lambda e, kc=kc, sq=sq: e.activation(out=sq[:, 0:n], in_=xt[:, kc, 0:n], func=AF.Square),
             (b_x,), (b_sq,))
        s.op("pe", lambda e, kc=kc, sq=sq: e.matmul(pss[:, 0:n], ones_f[:, :], sq[:, 0:n], start=(kc == 0),
                                                   stop=(kc == KC - 1)),
             (b_sq, b_ones), (b_pss,), inc=True)
    rs, b_rs = K["rs"].get()
    s.op("act", lambda e: e.activation(out=rs[:, 0:n], in_=pss[:, 0:n], func=AF.Sqrt, bias=eps_t[:, 0:1],
                                       scale=1.0 / D), (b_pss, b_eps), (b_rs,))
    s.op("dve", lambda e: e.reciprocal(out=rs[:, 0:n], in_=rs[:, 0:n]), (b_rs,), (b_rs,))
    for kc in range(KC):
        tm, b_tm = K["sq"].get()
        s.op("dve", lambda e, kc=kc, tm=tm: e.scalar_tensor_tensor(
            out=tm[:, 0:n], in0=xt[:, kc, 0:n], scalar=A[:, j, kc:kc + 1], in1=rs[:, 0:n], op0=ALU.mult,
            op1=ALU.mult), (b_x, b_A, b_rs), (b_tm,))
        s.op("act", lambda e, kc=kc, tm=tm: e.activation(
            out=ht[:, kc, 0:n], in_=tm[:, 0:n], func=AF.Identity, bias=mv[:, shi, kc:kc + 1], scale=1.0),
            (b_tm, b_mv), (b_h,))


def tok_tiles(nlat, nctx, step):
    tl = [(t0, min(step, nlat - t0), False) for t0 in range(0, nlat, step)]
    tl += [(nlat + t0, min(step, nctx - t0), True) for t0 in range(0, nctx, step)]
    return tl


def build_N():
    c = Ctx("progN")
    s = c.s
    xT = c.dram_in("xT", [128, KC, TS])
    gn_d = c.dram_in("gn", [128, KC])
    mv_d = c.dram_in("modv", [128, 6, KC])
    hT = c.dram_out("hT", [128, KC, TS], BF16)
    K = {}
    K["ones"], K["eps"] = _const_tiles(c)
    K["AB"] = emit_modprep(c, gn_d, mv_d, "n")
    K["psrot"] = PsRot(c, 2)
    K["sq"] = Rot(c, "sq", [128, 512], F32, 3)
    K["rs"] = Rot(c, "rs", [128, 512], F32, 2)
    xs = Rot(c, "x", [128, KC, 512], F32, 1)
    hs = Rot(c, "h", [128, KC, 512], BF16, 2)
    for (t0, n, is_ctx) in tok_tiles(TSH, CSH, 512):
        xt, b_x = xs.get()
        ht, b_h = hs.get()
        s.dma(xt[:, :, 0:n], xT[:, :, t0:t0 + n], b_x, True)
        emit_norm(c, K, xt, b_x, n, is_ctx, ht, b_h)
        s.dma(hT[:, :, t0:t0 + n], ht[:, :, 0:n], b_h, False)
    return c.close()


NT23 = 256


def build_L23():
    c = Ctx("progL23")
    s = c.s
    xT = c.dram_in("xT", [128, KC, TS])
    hT = c.dram_in("hT", [128, KC, TS], BF16)
    brT = c.dram_in("brT", [128, 32, TS], BF16)
    wmg = c.dram_in("wmg", [128, KC, 4 * D])
    wbr = c.dram_in("wbr", [128, 32, D])
    wout = c.dram_in("wout", [128, KC, D])
    mv_d = c.dram_in("modv", [128, 6, KC])
    gnn_d = c.dram_in("gn_next", [128, KC])
    mvn_d = c.dram_in("modv_next", [128, 6, KC])
    xo = c.dram_out("xo", [128, KC, TS])
    ho = c.dram_out("ho", [128, KC, TS], BF16)
    K = {}
    K["ones"], K["eps"] = _const_tiles(c)
    K["AB"] = emit_modprep(c, gnn_d, mvn_d, "nx")
    mv, b_mv = c.sb("mvcur", [128, 6, KC])
    s.dma(mv[:], mv_d[:, :, :], b_mv, True)
    psr = PsRot(c, 8)
    K["psrot"] = psr
    K["sq"] = Rot(c, "sq", [128, NT23], F32, 3)
    K["rs"] = Rot(c, "rs", [128, NT23], F32, 2)
    gsr = Rot(c, "gs", [128, NT23], F32, 3)
    accr = Rot(c, "acc", [128, NT23], F32, 2)
    tmr = Rot(c, "tmm", [128, NT23], F32, 3)
    wp = WPool(c, KC, 128)
    xs = Rot(c, "x", [128, KC, NT23], F32, 1)
    hs = Rot(c, "h", [128, KC, NT23], BF16, 1)
    bs = Rot(c, "br", [128, 32, NT23], BF16, 1)
    ms = Rot(c, "mg", [128, KC, NT23], BF16, 1)
    hns = Rot(c, "hn", [128, KC, NT23], BF16, 1)
    for (t0, n, is_ctx) in tok_tiles(TSH, CSH, NT23):
        xt, b_x = xs.get()
        ht, b_h = hs.get()
        bt, b_b = bs.get()
        mt, b_m = ms.get()
        s.dma(xt[:, :, 0:n], xT[:, :, t0:t0 + n], b_x, True)
        s.dma(ht[:, :, 0:n], hT[:, :, t0:t0 + n], b_h, True)
        s.dma(bt[:, :, 0:n], brT[:, :, t0:t0 + n], b_b, True)
        for ch in range(KC):
            acc, b_acc = accr.get()
            for i in range(4):
                w, b_w = wp.get(wmg[:, :, i * D + ch * 128:i * D + (ch + 1) * 128], KC, 128)
                pg, b_pg = psr.get()
                for kc in range(KC):
                    s.op("pe", lambda e, kc=kc, w=w, pg=pg: e.matmul(pg[:, 0:n], w[:, kc, :], ht[:, kc, 0:n],
                                                                   start=(kc == 0), stop=(kc == KC - 1)),
                         (b_w, b_h), (b_pg,), inc=(kc == KC - 1))
                gs, b_gs = gsr.get()
                s.op("act", lambda e, gs=gs, pg=pg: e.activation(out=gs[:, 0:n], in_=pg[:, 0:n], func=AF.Sigmoid),
                     (b_pg,), (b_gs,))
                w2, b_w2 = wp.get(wbr[:, i * 8:(i + 1) * 8, ch * 128:(ch + 1) * 128], 8, 128)
                pp, b_pp = psr.get()
                for kk in range(8):
                    s.op("pe", lambda e, kk=kk, w2=w2, pp=pp, i=i: e.matmul(
                        pp[:, 0:n], w2[:, kk, :], bt[:, i * 8 + kk, 0:n], start=(kk == 0), stop=(kk == 7)),
                        (b_w2, b_b), (b_pp,), inc=(kk == 7))
                if i == 0:
                    s.op("dve", lambda e, gs=gs, pp=pp, acc=acc: e.tensor_tensor(
                        out=acc[:, 0:n], in0=pp[:, 0:n], in1=gs[:, 0:n], op=ALU.mult), (b_pp, b_gs), (b_acc,))
                else:
                    tm, b_tm = tmr.get()
                    s.op("dve", lambda e, gs=gs, pp=pp, tm=tm: e.tensor_tensor(
                        out=tm[:, 0:n], in0=pp[:, 0:n], in1=gs[:, 0:n], op=ALU.mult), (b_pp, b_gs), (b_tm,))
                    s.op("pool", lambda e, tm=tm, acc=acc: e.tensor_tensor(
                        out=acc[:, 0:n], in0=acc[:, 0:n], in1=tm[:, 0:n], op=ALU.add), (b_tm, b_acc), (b_acc,))
            s.op("act", lambda e, ch=ch, acc=acc: e.activation(out=mt[:, ch, 0:n], in_=acc[:, 0:n], func=AF.Copy),
                 (b_acc,), (b_m,))
        gti = 5 if is_ctx else 2
        for ch in range(KC):
            w, b_w = wp.get(wout[:, :, ch * 128:(ch + 1) * 128], KC, 128)
            po, b_po = psr.get()
            for kc in range(KC):
                s.op("pe", lambda e, kc=kc, w=w, po=po: e.matmul(po[:, 0:n], w[:, kc, :], mt[:, kc, 0:n],
                                                               start=(kc == 0), stop=(kc == KC - 1)),
                     (b_w, b_m), (b_po,), inc=(kc == KC - 1))
            s.op("dve", lambda e, ch=ch, po=po: e.scalar_tensor_tensor(
                out=xt[:, ch, 0:n], in0=po[:, 0:n], scalar=mv[:, gti, ch:ch + 1], in1=xt[:, ch, 0:n],
                op0=ALU.mult, op1=ALU.add), (b_po, b_mv, b_x), (b_x,))
        s.dma(xo[:, :, t0:t0 + n], xt[:, :, 0:n], b_x, False)
        hn, b_hn = hns.get()
        emit_norm(c, K, xt, b_x, n, is_ctx, hn, b_hn)
        s.dma(ho[:, :, t0:t0 + n], hn[:, :, 0:n], b_hn, False)
    return c.close()


C_AQ, C_AK, C_AV, C_AG = 0, 256, 384, 512
C_BQ, C_BK, C_BV, C_BG = 768, 1024, 1280, 1536
C_CQ, C_CKV, C_CPE, C_CG = 1792, 2688, 3200, 3264
C_DQ, C_DK, C_DV, C_DG = 3520, 3776, 4032, 4288
W1COLS = 4544
G_QNA, G_KNA, G_QNB, G_KNB, G_QNN, G_KNN, G_QNP, G_KNP, G_QND, G_KND, G_SUB = range(11)
G_GQA = 11
G_GKVA = 18
G_SINK = 22
G_LQ1, G_LK1, G_LQ2, G_LK2 = 24, 25, 26, 27
G_LINIT, G_1MLI = 28, 29
NGV = 30
NKT = TB // 128
NEG = -200.0


def na_patterns():
    rows, W, KH, KWn = SEQ // GRID_W, GRID_W, 8, 16
    w = np.arange(W)
    col_start = np.clip(w - KWn // 2, 0, W - KWn)
    col_ok = (w[None, :] >= col_start[:, None]) & (w[None, :] < col_start[:, None] + KWn)
    dc = np.clip(w[None, :] - w[:, None], -(KWn - 1), KWn - 1) + (KWn - 1)
    pairs, pats, keys = {}, [], {}
    for m in range(rows // 2):
        for kt in range(rows // 2):
            valid = np.zeros((128, 128), bool)
            dri = np.zeros((128, 128), np.int64)
            dci = np.zeros((128, 128), np.int64)
            for rl in range(2):
                r = 2 * m + rl
                rs = min(max(r - KH // 2, 0), rows - KH)
                for kl in range(2):
                    kr = 2 * kt + kl
                    ok_row = rs <= kr < rs + KH
                    blk = (slice(kl * 64, kl * 64 + 64), slice(rl * 64, rl * 64 + 64))
                    valid[blk] = (col_ok.T if ok_row else False)
                    dri[blk] = min(max(kr - r + (KH - 1), 0), 2 * KH - 2)
                    dci[blk] = dc.T
            if not valid.any():
                continue
            key = (valid.tobytes(), (dri * valid).tobytes())
            if key not in keys:
                keys[key] = len(pats)
                pats.append((valid, dri, dci))
            pairs[(m, kt)] = keys[key]
    return pairs, pats


def build_L1():
    NA_PAIRS, NA_PATS = na_patterns()
    NPAT = len(NA_PATS)
    c = Ctx("progL1")
    s = c.s
    nc = c.nc
    hT = c.dram_in("hT", [128, KC, TB], BF16)
    W1 = c.dram_in("W1", [128, KC, W1COLS])
    wqu_d = c.dram_in("wqu", [128, 7, 384])
    wkvu_d = c.dram_in("wkvu", [128, 4, 512])
    gvec_d = c.dram_in("gvec", [128, NGV])
    ropeA_d = c.dram_in("ropeA", [2, 128, SEQ])
    ropeD_d = c.dram_in("ropeD", [2, 128, SEQ])
    RT_d = c.dram_in("RT", [2, 128, 128])
    tri_d = c.dram_in("tri", [128, 2, 256])
    biasB_d = c.dram_in("biasB", [128, NPAT, 2, 128])
    brT = c.dram_out("brT", [128, 8, TB], BF16)

    def scratch(name, shape):
        return nc.dram_tensor(name, list(shape), BF16).ap()

    qA, kA, vA, gA = scratch("qA", [128, 2, TB]), scratch("kA", [128, TB]), scratch("vA", [TB, 128]), scratch("gA", [128, 2, TB])
    qB, kB, vB, gB = scratch("qB", [128, 2, TB]), scratch("kB", [128, 2, TB]), scratch("vB", [TB, 256]), scratch("gB", [128, 2, TB])
    qCn, qCp = scratch("qCn", [128, 2, TB]), scratch("qCp", [64, 2, TB])
    kCn, kCp = scratch("kCn", [128, 2, TB]), scratch("kCp", [64, TB])
    vC, gC = scratch("vC", [TB, 256]), scratch("gC", [128, 2, TB])
    qD, kD, vD, gD = scratch("qD", [128, 2, TB]), scratch("kD", [128, 2, TB]), scratch("vD", [TB, 256]), scratch("gD", [128, 2, TB])

    (ones_f, b_ones), (eps_t, b_eps) = _const_tiles(c)
    onesb, b_onesb = c.sb("onesb", [128, 128], F32)
    ones_h, b_onesh = c.sb("ones_h", [128, 128], BF16)
    s.op("dve", lambda e: e.memset(onesb[:], 0.0), (), (b_onesb,))
    s.op("dve", lambda e: e.memset(onesb[0:64, 0:64], 1.0), (), (b_onesb,))
    s.op("dve", lambda e: e.memset(onesb[64:128, 64:128], 1.0), (), (b_onesb,))
    s.op("dve", lambda e: e.memset(ones_h[:], 1.0), (), (b_onesh,))
    gv, b_gv = c.sb("gv", [128, NGV])
    RT, b_RT = c.sb("RT", [128, 2, 128])
    s.dma(gv[:], gvec_d[:, :], b_gv, True)
    s.dma(RT[:, 0, :], RT_d[0, :, :], b_RT, True)
    s.dma(RT[:, 1, :], RT_d[1, :, :], b_RT, True)
    wqu, b_wqu = c.sb("wqu_bf", [128, 7, 384], BF16)
    wkvu, b_wkvu = c.sb("wkvu_bf", [128, 4, 512], BF16)
    wvu, b_wvu = c.sb("wvu_bf", [128, 4, 256], BF16)
    with ExitStack() as es0:
        st = es0.enter_context(nc.sbuf_tensor("upst", [128, 7, 384], F32))
        b_st = s.buf("upst")
        s.dma(st[:], wqu_d[:, :, :], b_st, True)
        s.op("pool", lambda e: e.tensor_copy(out=wqu[:], in_=st[:]), (b_st,), (b_wqu,))
        s.dma(st[:, 0:4, :], wkvu_d[:, :, 0:384], b_st, True)
        s.op("pool", lambda e: e.tensor_copy(out=wkvu[:, :, 0:384], in_=st[:, 0:4, :]), (b_st,), (b_wkvu,))
        s.dma(st[:, 0:4, 0:128], wkvu_d[:, :, 384:512], b_st, True)
        s.op("pool", lambda e: e.tensor_copy(out=wkvu[:, :, 384:512], in_=st[:, 0:4, 0:128]), (b_st,), (b_wkvu,))
        for hh in range(2):
            s.op("pool", lambda e, hh=hh: e.tensor_copy(out=wvu[:, :, hh * 128:(hh + 1) * 128],
                                                        in_=wkvu[:, :, hh * 256 + 128:hh * 256 + 256]),
                 (b_wkvu,), (b_wvu,))
        s.sync_all()
    lam, b_lam = c.sb("lam", [128, 4])
    es_l = ExitStack()
    psl = es_l.enter_context(nc.psum_tensor("ps_lam", [128, 512], F32))
    b_psl = s.buf("ps_lam", excl=True)
    s.op("dve", lambda e: e.tensor_tensor(out=lam[:, 0:1], in0=gv[:, G_LQ1:G_LQ1 + 1], in1=gv[:, G_LK1:G_LK1 + 1],
                                          op=ALU.mult), (b_gv,), (b_lam,))
    s.op("dve", lambda e: e.tensor_tensor(out=lam[:, 1:2], in0=gv[:, G_LQ2:G_LQ2 + 1], in1=gv[:, G_LK2:G_LK2 + 1],
                                          op=ALU.mult), (b_gv, b_lam), (b_lam,))
    s.op("pe", lambda e: e.matmul(psl[:, 0:2], ones_f[:, :], lam[:, 0:2], start=True, stop=True),
         (b_ones, b_lam), (b_psl,))
    s.op("act", lambda e: e.activation(out=lam[:, 2:4], in_=psl[:, 0:2], func=AF.Exp), (b_psl, b_lam), (b_lam,))
    s.op("dve", lambda e: e.tensor_tensor(out=lam[:, 0:1], in0=lam[:, 2:3], in1=lam[:, 3:4], op=ALU.subtract),
         (b_lam,), (b_lam,))
    s.op("dve", lambda e: e.tensor_tensor(out=lam[:, 0:1], in0=lam[:, 0:1], in1=gv[:, G_LINIT:G_LINIT + 1],
                                          op=ALU.add), (b_lam, b_gv), (b_lam,))
    s.op("dve", lambda e: e.tensor_scalar(out=lam[:, 1:2], in0=lam[:, 0:1], scalar1=-1.0, scalar2=None,
                                          op0=ALU.mult), (b_lam,), (b_lam,))
    esk, b_esk = c.sb("esk", [128, 2])
    s.op("act", lambda e: e.activation(out=esk[:], in_=gv[:, G_SINK:G_SINK + 2], func=AF.Exp), (b_gv,), (b_esk,))
    subg, b_subg = c.sb("subg", [128, 1])
    s.op("dve", lambda e: e.tensor_tensor(out=subg[:], in0=gv[:, G_SUB:G_SUB + 1], in1=gv[:, G_1MLI:G_1MLI + 1],
                                          op=ALU.mult), (b_gv,), (b_subg,))
    s.sync_all()
    es_l.close()
    import os
    if os.environ.get("L1STOP", "") == "0":
        return c.close()

    NT = 512
    tiles = tok_tiles(SEQ, CTX, NT)
    import os
    DBG = dict(kv.split("=") for kv in os.environ.get("L1DBG", "").split(",") if kv)
    ptiles = tiles[:int(DBG["ptiles"])] + tiles[-1:] if "ptiles" in DBG else tiles
    TMIX = DBG.get("T", "CDAB")

    with ExitStack() as esP:
        c_stack_save = c.stack
        c.stack = esP
        s.stack = esP
        psr = PsRot(c, 8, "pp")
        ycq, b_ycq = c.sb("ycq", [128, 7, NT], F32)
        cqn, b_cqn = c.sb("cqn", [128, 7, NT], BF16)
        ckvn, b_ckvn = c.sb("ckvn", [128, 4, NT], BF16)
        hts = Rot(c, "hT", [128, KC, NT], BF16, 2)
        wp = WPool(c, KC, 128, 2, 2)
        sqr = Rot(c, "sq", [128, NT], F32, 3)
        rsr = Rot(c, "rs", [128, NT], F32, 3)
        zr = Rot(c, "z", [128, NT], F32, 3)
        t1r = Rot(c, "t1", [128, NT], F32, 3)
        outr = Rot(c, "o", [128, NT], BF16, 6)
        vor = Rot(c, "vo", [128, 256], BF16, 4)
        ropA = Rot(c, "ropA", [128, 2, NT], F32, 1)
        ropD = Rot(c, "ropD", [128, 2, NT], F32, 1)
        for rot_ in (sqr, zr):
            for (t_, b_) in rot_.t:
                s.op("dve", lambda e, t_=t_: e.memset(t_[:], 0.0), (), (b_,))

        def proj(ht, b_h, n, col0, M=128):
            w, b_w = wp.get(W1[:, :, col0:col0 + M], KC, M)
            p, b_p = psr.get()
            for kc in range(KC):
                s.op("pe", lambda e, kc=kc: e.matmul(p[0:M, 0:n], w[:, kc, 0:M], ht[:, kc, 0:n], start=(kc == 0),
                                                     stop=(kc == KC - 1)), (b_w, b_h), (b_p,), inc=(kc == KC - 1))
            return p, b_p

        def proj_tm(ht, b_h, n, col0, ncols, dst, dcol0):
            w, b_w = wp.get(W1[:, :, col0:col0 + ncols], KC, ncols)
            for sub in range(n // 128):
                p, b_p = psr.get()
                for kc in range(KC):
                    s.op("pe", lambda e, kc=kc, sub=sub, p=p: e.matmul(
                        p[:, 0:ncols], ht[:, kc, sub * 128:(sub + 1) * 128], w[:, kc, 0:ncols], start=(kc == 0),
                        stop=(kc == KC - 1)), (b_w, b_h), (b_p,), inc=(kc == KC - 1))
                vo, b_vo = vor.get()
                s.op("act", lambda e, p=p, vo=vo: e.activation(out=vo[:, 0:ncols], in_=p[:, 0:ncols], func=AF.Copy),
                     (b_p,), (b_vo,))
                s.dma(dst(sub)[:, dcol0:dcol0 + ncols], vo[:, 0:ncols], b_vo, False)

        def rstd_of(p, b_p, P, n, d, onesm, b_onesm):
            sq, b_sq = sqr.get()
            s.op("act", lambda e: e.activation(out=sq[0:P, 0:n], in_=p[0:P, 0:n], func=AF.Square), (b_p,), (b_sq,))
            ps2, b_ps2 = psr.get()
            s.op("pe", lambda e: e.matmul(ps2[:, 0:n], onesm[:, :], sq[:, 0:n], start=True, stop=True),
                 (b_sq, b_onesm), (b_ps2,))
            rs, b_rs = rsr.get()
            s.op("act", lambda e: e.activation(out=rs[0:P, 0:n], in_=ps2[0:P, 0:n], func=AF.Sqrt,
                                               bias=eps_t[0:P, 0:1], scale=1.0 / d), (b_ps2, b_eps), (b_rs,))
            s.op("dve", lambda e: e.reciprocal(out=rs[0:P, 0:n], in_=rs[0:P, 0:n]), (b_rs,), (b_rs,))
            return rs, b_rs

        def headnorm(p, b_p, P, n, d, gcol, rope, dst_ap):
            onesm, b_onesm = (ones_f, b_ones) if d == 128 else (onesb, b_onesb)
            rs, b_rs = rstd_of(p, b_p, P, n, d, onesm, b_onesm)
            o, b_o = outr.get()
            if rope is None:
                s.op("dve", lambda e: e.scalar_tensor_tensor(
                    out=o[0:P, 0:n], in0=p[0:P, 0:n], scalar=gv[0:P, gcol:gcol + 1], in1=rs[0:P, 0:n],
                    op0=ALU.mult, op1=ALU.mult), (b_p, b_gv, b_rs), (b_o,))
            else:
                rt, b_rt, ri = rope
                z, b_z = zr.get()
                s.op("dve", lambda e: e.scalar_tensor_tensor(
                    out=z[0:P, 0:n], in0=p[0:P, 0:n], scalar=gv[0:P, gcol:gcol + 1], in1=rs[0:P, 0:n],
                    op0=ALU.mult, op1=ALU.mult), (b_p, b_gv, b_rs), (b_z,))
                pr, b_pr = psr.get()
                s.op("pe", lambda e: e.matmul(pr[:, 0:n], RT[:, ri, :], z[:, 0:n], start=True, stop=True),
                     (b_RT, b_z), (b_pr,))
                t1, b_t1 = t1r.get()
                s.op("pool", lambda e: e.tensor_tensor(out=t1[0:P, 0:n], in0=z[0:P, 0:n], in1=rt[0:P, 0, 0:n],
                                                       op=ALU.mult), (b_z, b_rt), (b_t1,))
                t2, b_t2 = t1r.get()
                s.op("dve", lambda e: e.tensor_tensor(out=t2[0:P, 0:n], in0=pr[0:P, 0:n], in1=rt[0:P, 1, 0:n],
                                                      op=ALU.mult), (b_pr, b_rt), (b_t2,))
                s.op("dve", lambda e: e.tensor_tensor(out=o[0:P, 0:n], in0=t1[0:P, 0:n], in1=t2[0:P, 0:n],
                                                      op=ALU.add), (b_t1, b_t2), (b_o,))
            s.dma(dst_ap, o[0:P, 0:n], b_o, False)

        def gate(ht, b_h, n, col0, dst_ap):
            p, b_p = proj(ht, b_h, n, col0)
            o, b_o = outr.get()
            s.op("act", lambda e: e.activation(out=o[:, 0:n], in_=p[:, 0:n], func=AF.Silu), (b_p,), (b_o,))
            s.dma(dst_ap, o[:, 0:n], b_o, False)

        for (t0, n, is_ctx) in ptiles:
            ht, b_h = hts.get()
            s.dma(ht[:, :, 0:n], hT[:, :, t0:t0 + n], b_h, True)
            tsl = slice(t0, t0 + n)
            ropeA = ropeD = ropeC = None
            if not is_ctx:
                ra, b_ra = ropA.get()
                rd, b_rd = ropD.get()
                for j in range(2):
                    s.dma(ra[:, j, 0:n], ropeA_d[j, :, t0:t0 + n], b_ra, True)
                    s.dma(rd[:, j, 0:n], ropeD_d[j, :, t0:t0 + n], b_rd, True)
                ropeA, ropeD, ropeC = (ra, b_ra, 0), (rd, b_rd, 1), (rd, b_rd, 1)

            def vdst(tensor):
                return lambda sub: tensor[t0 + sub * 128:t0 + (sub + 1) * 128, :]

            for hh in range(2):
                p, b_p = proj(ht, b_h, n, C_AQ + hh * 128)
                headnorm(p, b_p, 128, n, 128, G_QNA, ropeA, qA[:, hh, tsl])
                gate(ht, b_h, n, C_AG + hh * 128, gA[:, hh, tsl])
            p, b_p = proj(ht, b_h, n, C_AK)
            headnorm(p, b_p, 128, n, 128, G_KNA, ropeA, kA[:, tsl])
            proj_tm(ht, b_h, n, C_AV, 128, vdst(vA), 0)
            if os.environ.get("L1STOP", "") == "1":
                continue
            for hh in range(2):
                p, b_p = proj(ht, b_h, n, C_BQ + hh * 128)
                headnorm(p, b_p, 128, n, 128, G_QNB, None, qB[:, hh, tsl])
                p, b_p = proj(ht, b_h, n, C_BK + hh * 128)
                headnorm(p, b_p, 128, n, 128, G_KNB, None, kB[:, hh, tsl])
                gate(ht, b_h, n, C_BG + hh * 128, gB[:, hh, tsl])
                proj_tm(ht, b_h, n, C_BV + hh * 128, 128, vdst(vB), hh * 128)
            for hh in range(2):
                p, b_p = proj(ht, b_h, n, C_DQ + hh * 128)
                headnorm(p, b_p, 128, n, 64, G_QND, ropeD, qD[:, hh, tsl])
                p, b_p = proj(ht, b_h, n, C_DK + hh * 128)
                headnorm(p, b_p, 128, n, 64, G_KND, ropeD, kD[:, hh, tsl])
                gate(ht, b_h, n, C_DG + hh * 128, gD[:, hh, tsl])
                proj_tm(ht, b_h, n, C_DV + hh * 128, 128, vdst(vD), hh * 128)
            if os.environ.get("L1STOP", "") == "2":
                continue
            for hh in range(2):
                gate(ht, b_h, n, C_CG + hh * 128, gC[:, hh, tsl])
            for lat, (col0, nch, ysb, b_ysb, dstn, b_dstn, gc0, dim) in enumerate((
                    (C_CQ, 7, ycq, b_ycq, cqn, b_cqn, G_GQA, 896.0),
                    (C_CKV, 4, ycq, b_ycq, ckvn, b_ckvn, G_GKVA, 512.0))):
                if "l" in os.environ.get("L1SKIP", ""):
                    continue
                lacc, b_lacc = zr.get()
                for j in range(nch):
                    p, b_p = proj(ht, b_h, n, col0 + j * 128)
                    if j == 0:
                        s.op("act", lambda e, p=p: e.activation(out=lacc[:, 0:n], in_=p[:, 0:n], func=AF.Square),
                             (b_p,), (b_lacc,))
                    else:
                        sq, b_sq = sqr.get()
                        s.op("act", lambda e, p=p, sq=sq: e.activation(out=sq[:, 0:n], in_=p[:, 0:n],
                                                                       func=AF.Square), (b_p,), (b_sq,))
                        s.op("pool", lambda e, sq=sq: e.tensor_tensor(out=lacc[:, 0:n], in0=lacc[:, 0:n],
                                                                      in1=sq[:, 0:n], op=ALU.add),
                             (b_sq, b_lacc), (b_lacc,))
                    if "c" not in os.environ.get("L1SKIP", ""):
                        s.op("dve", lambda e, p=p, j=j: e.tensor_scalar(out=ysb[:, j, 0:n], in0=p[:, 0:n], scalar1=1.0,
                                                                        scalar2=None, op0=ALU.mult),
                             (b_p,), (b_ysb,))
                pss, b_pss = psr.get()
                s.op("pe", lambda e: e.matmul(pss[:, 0:n], ones_f[:, :], lacc[:, 0:n], start=True, stop=True),
                     (b_lacc, b_ones), (b_pss,))
                rs, b_rs = rsr.get()
                s.op("act", lambda e, rs=rs, dim=dim: e.activation(out=rs[:, 0:n], in_=pss[:, 0:n], func=AF.Sqrt,
                                                                   bias=eps_t[:, 0:1], scale=1.0 / dim),
                     (b_pss, b_eps), (b_rs,))
                s.op("dve", lambda e, rs=rs: e.reciprocal(out=rs[:, 0:n], in_=rs[:, 0:n]), (b_rs,), (b_rs,))
                for j in range(0 if "s" in os.environ.get("L1SKIP", "") else nch):
                    s.op("dve", lambda e, j=j, rs=rs: e.scalar_tensor_tensor(
                        out=dstn[:, j, 0:n], in0=ysb[:, j, 0:n], scalar=gv[:, gc0 + j:gc0 + j + 1], in1=rs[:, 0:n],
                        op0=ALU.mult, op1=ALU.mult), (b_ysb, b_gv, b_rs), (b_dstn,))
            SKIP = os.environ.get("L1SKIP", "")
            for hh in range(2):
                if "n" in SKIP:
                    continue
                p, b_p = psr.get()
                for j in range(7):
                    s.op("pe", lambda e, j=j, p=p: e.matmul(p[:, 0:n], wqu[:, j, hh * 192:hh * 192 + 128],
                                                            cqn[:, j, 0:n], start=(j == 0), stop=(j == 6)),
                         (b_wqu, b_cqn), (b_p,), inc=(j == 6))
                headnorm(p, b_p, 128, n, 128, G_QNN, None, qCn[:, hh, tsl])
                if "p" in SKIP:
                    continue
                p, b_p = psr.get()
                for j in range(7):
                    s.op("pe", lambda e, j=j, p=p: e.matmul(p[0:64, 0:n], wqu[:, j, hh * 192 + 128:hh * 192 + 192],
                                                            cqn[:, j, 0:n], start=(j == 0), stop=(j == 6)),
                         (b_wqu, b_cqn), (b_p,), inc=(j == 6))
                headnorm(p, b_p, 64, n, 64, G_QNP, ropeC, qCp[:, hh, tsl])
                if "k" in SKIP:
                    continue
                p, b_p = psr.get()
                for j in range(4):
                    s.op("pe", lambda e, j=j, p=p: e.matmul(p[:, 0:n], wkvu[:, j, hh * 256:hh * 256 + 128],
                                                            ckvn[:, j, 0:n], start=(j == 0), stop=(j == 3)),
                         (b_wkvu, b_ckvn), (b_p,), inc=(j == 3))
                headnorm(p, b_p, 128, n, 128, G_KNN, None, kCn[:, hh, tsl])
            for sub in range(0 if "v" in SKIP else n // 128):
                p, b_p = psr.get()
                for j in range(4):
                    s.op("pe", lambda e, j=j, p=p, sub=sub: e.matmul(
                        p[:, 0:256], ckvn[:, j, sub * 128:(sub + 1) * 128], wvu[:, j, :], start=(j == 0),
                        stop=(j == 3)), (b_wvu, b_ckvn), (b_p,), inc=(j == 3))
                vo, b_vo = vor.get()
                s.op("act", lambda e, p=p, vo=vo: e.activation(out=vo[:, 0:256], in_=p[:, 0:256], func=AF.Copy),
                     (b_p,), (b_vo,))
                s.dma(vC[t0 + sub * 128:t0 + (sub + 1) * 128, :], vo[:, 0:256], b_vo, False)
            if "r" not in SKIP:
                p, b_p = proj(ht, b_h, n, C_CPE, M=64)
                headnorm(p, b_p, 64, n, 64, G_KNP, ropeC, kCp[:, tsl])
        s.sync_all()
        c.stack = c_stack_save
        s.stack = c_stack_save
    s.sync_all()
    build_L1_phaseT(c, locals())
    return c.close()


def build_L1_phaseT(c, L):
    s = c.s
    nc = c.nc
    brT = L["brT"]
    ones_f, b_ones, eps_t, b_eps = L["ones_f"], L["b_ones"], L["eps_t"], L["b_eps"]
    ones_h, b_onesh = L["ones_h"], L["b_onesh"]
    lam, b_lam, esk, b_esk, subg, b_subg = L["lam"], L["b_lam"], L["esk"], L["b_esk"], L["subg"], L["b_subg"]
    NA_PAIRS, NPAT = L["NA_PAIRS"], L["NPAT"]
    tiles = L["tiles"]
    NT = 512
    with ExitStack() as esT:
        save = c.stack
        c.stack = esT
        s.stack = esT
        srot = PsRot(c, 4, "pS")
        Ob = [c.ps("pO%d" % i) for i in range(2)]
        Db = [c.ps("pD%d" % i) for i in range(2)]
        prot = Rot(c, "P", [128, NT], BF16, 6)
        fin = Rot(c, "fin", [128, NT], F32, 6)
        outr = Rot(c, "ob", [128, NT], BF16, 3)
        vv, b_vv = c.sb("vv", [128, NKT, 256], BF16)

        def vview(v_d, ncols):
            return v_d.rearrange("(kt p) d -> p kt d", p=128)

        def finish(O, b_O, Dn, b_D, n, w0, g_ap, b_g, dst_ap, extra=None):
            rd, b_rd = fin.get()
            if extra is None:
                s.op("dve", lambda e: e.reciprocal(out=rd[:, 0:n], in_=Dn[:, w0:w0 + n]), (b_D,), (b_rd,))
            else:
                s.op("dve", lambda e: e.tensor_scalar(out=rd[:, 0:n], in0=Dn[:, w0:w0 + n], scalar1=extra,
                                                      scalar2=None, op0=ALU.add), (b_D, b_esk), (b_rd,))
                s.op("dve", lambda e: e.reciprocal(out=rd[:, 0:n], in_=rd[:, 0:n]), (b_rd,), (b_rd,))
            o, b_o = fin.get()
            s.op("dve", lambda e: e.tensor_tensor(out=o[:, 0:n], in0=O[:, w0:w0 + n], in1=rd[:, 0:n], op=ALU.mult),
                 (b_O, b_rd), (b_o,))
            ob, b_ob = outr.get()
            s.op("pool", lambda e: e.tensor_tensor(out=ob[:, 0:n], in0=o[:, 0:n], in1=g_ap, op=ALU.mult),
                 (b_o, b_g), (b_ob,))
            s.dma(dst_ap, ob[:, 0:n], b_ob, False)

        with ExitStack() as esC:
          if "C" in L["TMIX"]:
            c.stack = esC
            s.stack = esC
            kn, b_kn = c.sb("kn", [128, 2, TB], BF16)
            kp, b_kp = c.sb("kp", [64, TB], BF16)
            qnr = Rot(c, "qn", [128, 2, NT], BF16, 2)
            qpr = Rot(c, "qp", [64, 2, NT], BF16, 2)
            gr = Rot(c, "g", [128, 2, NT], BF16, 2)
            for hh in range(2):
                s.dma(kn[:, hh, :], L["kCn"][:, hh, :], b_kn, True)
            s.dma(kp[:, :], L["kCp"][:, :], b_kp, True)
            for kq in range(0, NKT, 22):
                s.dma(vv[:, kq:kq + 22, :], vview(L["vC"], 256)[:, kq:kq + 22, :], b_vv, True)
            scale = (128 + 64) ** -0.5
            it = 0
            for (t0, n, is_ctx) in tiles:
                tsl = slice(t0, t0 + n)
                qn, b_qn = qnr.get()
                qp, b_qp = qpr.get()
                g, b_g = gr.get()
                s.dma(qn[:, :, 0:n], L["qCn"][:, :, tsl], b_qn, True)
                s.dma(qp[:, :, 0:n], L["qCp"][:, :, tsl], b_qp, True)
                s.dma(g[:, :, 0:n], L["gC"][:, :, tsl], b_g, True)
                kts = [64, 65] if is_ctx else list(range(NKT))
                for hh in range(2):
                    O, b_O = Ob[it % 2]
                    Dn, b_D = Db[it % 2]
                    it += 1
                    for i, kt in enumerate(kts):
                        ksl = slice(kt * 128, (kt + 1) * 128)
                        S, b_S = srot.get()
                        s.op("pe", lambda e, S=S, ksl=ksl: e.matmul(S[:, 0:n], kn[:, hh, ksl], qn[:, hh, 0:n],
                                                                   start=True, stop=False),
                             (b_kn, b_qn), (b_S,), inc=False)
                        s.op("pe", lambda e, S=S, ksl=ksl: e.matmul(S[:, 0:n], kp[:, ksl], qp[:, hh, 0:n],
                                                                   start=False, stop=True),
                             (b_kp, b_qp), (b_S,))
                        P, b_P = prot.get()
                        s.op("act", lambda e, S=S, P=P: e.activation(out=P[:, 0:n], in_=S[:, 0:n], func=AF.Exp,
                                                                     scale=scale), (b_S,), (b_P,))
                        first, last = (i == 0), (i == len(kts) - 1)
                        s.op("pe", lambda e, P=P, kt=kt: e.matmul(O[:, 0:n], vv[:, kt, hh * 128:(hh + 1) * 128],
                                                                 P[:, 0:n], start=first, stop=last),
                             (b_vv, b_P), (b_O,), inc=False)
                        s.op("pe", lambda e, P=P: e.matmul(Dn[:, 0:n], ones_h[:, :], P[:, 0:n], start=first,
                                                          stop=last), (b_onesh, b_P), (b_D,))
                    finish(O, b_O, Dn, b_D, n, 0, g[:, hh, 0:n], b_g, brT[:, 2 * 2 + hh, tsl])
            s.sync_all()
        s.sync_all()

        with ExitStack() as esD:
          if "D" in L["TMIX"]:
            c.stack = esD
            s.stack = esD
            kd = [c.sb("kd%d" % i, [64, 2, TB], BF16) for i in range(2)]
            qdr = [Rot(c, "qd%d_" % i, [64, 2, NT], BF16, 2) for i in range(2)]
            gr = Rot(c, "g", [128, 2, NT], BF16, 2)
            for i in range(2):
                for hh in range(2):
                    s.dma(kd[i][0][:, hh, :], L["kD"][i * 64:(i + 1) * 64, hh, :], kd[i][1], True)
            for kq in range(0, NKT, 22):
                s.dma(vv[:, kq:kq + 22, :], vview(L["vD"], 256)[:, kq:kq + 22, :], b_vv, True)
            scale = 64 ** -0.5
            for (t0, n, is_ctx) in tiles:
                tsl = slice(t0, t0 + n)
                qd = [qdr[i].get() for i in range(2)]
                g, b_g = gr.get()
                for i in range(2):
                    s.dma(qd[i][0][:, :, 0:n], L["qD"][i * 64:(i + 1) * 64, :, tsl], qd[i][1], True)
                s.dma(g[:, :, 0:n], L["gD"][:, :, tsl], b_g, True)
                kts = [64, 65] if is_ctx else list(range(NKT))
                for hh in range(2):
                    for ki, kt in enumerate(kts):
                        ksl = slice(kt * 128, (kt + 1) * 128)
                        first, last = (ki == 0), (ki == len(kts) - 1)
                        for i in range(2):
                            S, b_S = srot.get()
                            s.op("pe", lambda e, S=S, i=i, ksl=ksl: e.matmul(
                                S[:, 0:n], kd[i][0][:, hh, ksl], qd[i][0][:, hh, 0:n], start=True, stop=True),
                                (kd[i][1], qd[i][1]), (b_S,))
                            P, b_P = prot.get()
                            s.op("act", lambda e, S=S, P=P: e.activation(out=P[:, 0:n], in_=S[:, 0:n], func=AF.Exp,
                                                                         scale=scale), (b_S,), (b_P,))
                            O, b_O = Ob[i]
                            Dn, b_D = Db[i]
                            s.op("pe", lambda e, P=P, kt=kt, O=O: e.matmul(
                                O[:, 0:n], vv[:, kt, hh * 128:(hh + 1) * 128], P[:, 0:n], start=first, stop=last),
                                (b_vv, b_P), (b_O,), inc=False)
                            s.op("pe", lambda e, P=P, Dn=Dn: e.matmul(Dn[:, 0:n], ones_h[:, :], P[:, 0:n],
                                                                     start=first, stop=last),
                                 (b_onesh, b_P), (b_D,))
                    ab = []
                    for i in range(2):
                        rd, b_rd = fin.get()
                        s.op("dve", lambda e, rd=rd, i=i: e.reciprocal(out=rd[:, 0:n], in_=Db[i][0][:, 0:n]),
                             (Db[i][1],), (b_rd,))
                        a, b_a = fin.get()
                        s.op("dve", lambda e, rd=rd, a=a, i=i: e.tensor_tensor(
                            out=a[:, 0:n], in0=Ob[i][0][:, 0:n], in1=rd[:, 0:n], op=ALU.mult),
                            (Ob[i][1], b_rd), (b_a,))
                        ab.append((a, b_a))
                    o, b_o = fin.get()
                    s.op("dve", lambda e, o=o: e.scalar_tensor_tensor(
                        out=o[:, 0:n], in0=ab[1][0][:, 0:n], scalar=lam[:, 1:2], in1=ab[0][0][:, 0:n],
                        op0=ALU.mult, op1=ALU.add), (ab[0][1], ab[1][1], b_lam), (b_o,))
                    sq, b_sq = fin.get()
                    s.op("act", lambda e, sq=sq, o=o: e.activation(out=sq[:, 0:n], in_=o[:, 0:n], func=AF.Square),
                         (b_o,), (b_sq,))
                    S, b_S = srot.get()
                    s.op("pe", lambda e, S=S, sq=sq: e.matmul(S[:, 0:n], ones_f[:, :], sq[:, 0:n], start=True,
                                                             stop=True), (b_ones, b_sq), (b_S,))
                    rs, b_rs = fin.get()
                    s.op("act", lambda e, rs=rs, S=S: e.activation(out=rs[:, 0:n], in_=S[:, 0:n], func=AF.Sqrt,
                                                                   bias=eps_t[:, 0:1], scale=1.0 / 128),
                         (b_S, b_eps), (b_rs,))
                    s.op("dve", lambda e, rs=rs: e.reciprocal(out=rs[:, 0:n], in_=rs[:, 0:n]), (b_rs,), (b_rs,))
                    o2, b_o2 = fin.get()
                    s.op("dve", lambda e, o2=o2, o=o, rs=rs: e.scalar_tensor_tensor(
                        out=o2[:, 0:n], in0=o[:, 0:n], scalar=subg[:, 0:1], in1=rs[:, 0:n], op0=ALU.mult,
                        op1=ALU.mult), (b_o, b_subg, b_rs), (b_o2,))
                    ob, b_ob = outr.get()
                    s.op("pool", lambda e, ob=ob, o2=o2: e.tensor_tensor(out=ob[:, 0:n], in0=o2[:, 0:n],
                                                                         in1=g[:, hh, 0:n], op=ALU.mult),
                         (b_o2, b_g), (b_ob,))
                    s.dma(brT[:, 3 * 2 + hh, tsl], ob[:, 0:n], b_ob, False)
            s.sync_all()
        s.sync_all()

        with ExitStack() as esA:
          if "A" in L["TMIX"]:
            c.stack = esA
            s.stack = esA
            ka, b_ka = c.sb("ka", [128, TB], BF16)
            tri, b_tri = c.sb("tri", [128, 2, 256], F32)
            qar = Rot(c, "qa", [128, 256], BF16, 3)
            gar = Rot(c, "ga", [128, 256], BF16, 3)
            s.dma(ka[:, :], L["kA"][:, :], b_ka, True)
            s.dma(tri[:], L["tri_d"][:, :, :], b_tri, True)
            for kq in range(0, NKT, 22):
                s.dma(vv[:, kq:kq + 22, 0:128], vview(L["vA"], 128)[:, kq:kq + 22, :], b_vv, True)
            scale = 128 ** -0.5
            for blk in range(NKT):
                bsl = slice(blk * 128, (blk + 1) * 128)
                qa, b_qa = qar.get()
                ga, b_ga = gar.get()
                for hh in range(2):
                    s.dma(qa[:, hh * 128:(hh + 1) * 128], L["qA"][:, hh, bsl], b_qa, True)
                    s.dma(ga[:, hh * 128:(hh + 1) * 128], L["gA"][:, hh, bsl], b_ga, True)
                if blk >= 64:
                    kts = [(64, None), (65, None)]
                else:
                    kts = []
                    if blk >= 1:
                        kts.append((blk - 1, 0))
                    kts.append((blk, None))
                    if blk <= 62:
                        kts.append((blk + 1, 1))
                    kts += [(64, None), (65, None)]
                O, b_O = Ob[blk % 2]
                Dn, b_D = Db[blk % 2]
                for ki, (kt, mk) in enumerate(kts):
                    ksl = slice(kt * 128, (kt + 1) * 128)
                    first, last = (ki == 0), (ki == len(kts) - 1)
                    S, b_S = srot.get()
                    s.op("pe", lambda e, S=S, ksl=ksl: e.matmul(S[:, 0:256], ka[:, ksl], qa[:, 0:256], start=True,
                                                               stop=True), (b_ka, b_qa), (b_S,))
                    P, b_P = prot.get()
                    s.op("act", lambda e, S=S, P=P: e.activation(out=P[:, 0:256], in_=S[:, 0:256], func=AF.Exp,
                                                                 scale=scale), (b_S,), (b_P,))
                    if mk is not None:
                        s.op("dve", lambda e, P=P, mk=mk: e.tensor_tensor(out=P[:, 0:256], in0=P[:, 0:256],
                                                                          in1=tri[:, mk, :], op=ALU.mult),
                             (b_P, b_tri), (b_P,))
                    s.op("pe", lambda e, P=P, kt=kt: e.matmul(O[:, 0:256], vv[:, kt, 0:128], P[:, 0:256], start=first,
                                                             stop=last), (b_vv, b_P), (b_O,), inc=False)
                    s.op("pe", lambda e, P=P: e.matmul(Dn[:, 0:256], ones_h[:, :], P[:, 0:256], start=first,
                                                      stop=last), (b_onesh, b_P), (b_D,))
                for hh in range(2):
                    finish(O, b_O, Dn, b_D, 128, hh * 128, ga[:, hh * 128:(hh + 1) * 128], b_ga,
                           brT[:, 0 * 2 + hh, bsl], extra=esk[:, hh:hh + 1])
            s.sync_all()
        s.sync_all()

        with ExitStack() as esB:
          if "B" in L["TMIX"]:
            c.stack = esB
            s.stack = esB
            kb, b_kb = c.sb("kb", [128, 2, TB], BF16)
            bias, b_bias = c.sb("biasB", [128, NPAT, 2, 128], F32)
            qbr = Rot(c, "qb", [128, 256], BF16, 3)
            gbr = Rot(c, "gb", [128, 256], BF16, 3)
            tmr = Rot(c, "tmb", [128, 128], F32, 3)
            for hh in range(2):
                s.dma(kb[:, hh, :], L["kB"][:, hh, :], b_kb, True)
            s.dma(bias[:], L["biasB_d"][:, :, :, :], b_bias, True)
            for kq in range(0, NKT, 22):
                s.dma(vv[:, kq:kq + 22, :], vview(L["vB"], 256)[:, kq:kq + 22, :], b_vv, True)
            scale = 128 ** -0.5
            it = 0
            for blk in range(NKT):
                bsl = slice(blk * 128, (blk + 1) * 128)
                qb, b_qb = qbr.get()
                gb, b_gb = gbr.get()
                for hh in range(2):
                    s.dma(qb[:, hh * 128:(hh + 1) * 128], L["qB"][:, hh, bsl], b_qb, True)
                    s.dma(gb[:, hh * 128:(hh + 1) * 128], L["gB"][:, hh, bsl], b_gb, True)
                if blk >= 64:
                    kts = [(64, None), (65, None)]
                else:
                    kts = [(kt, NA_PAIRS[(blk, kt)]) for kt in range(64) if (blk, kt) in NA_PAIRS]
                    kts += [(64, None), (65, None)]
                for hh in range(2):
                    O, b_O = Ob[it % 2]
                    Dn, b_D = Db[it % 2]
                    it += 1
                    for ki, (kt, pat) in enumerate(kts):
                        ksl = slice(kt * 128, (kt + 1) * 128)
                        first, last = (ki == 0), (ki == len(kts) - 1)
                        S, b_S = srot.get()
                        s.op("pe", lambda e, S=S, ksl=ksl: e.matmul(S[:, 0:128], kb[:, hh, ksl],
                                                                   qb[:, hh * 128:(hh + 1) * 128], start=True,
                                                                   stop=True), (b_kb, b_qb), (b_S,))
                        P, b_P = prot.get()
                        if pat is None:
                            s.op("act", lambda e, S=S, P=P: e.activation(out=P[:, 0:128], in_=S[:, 0:128],
                                                                         func=AF.Exp, scale=scale), (b_S,), (b_P,))
                        else:
                            tm, b_tm = tmr.get()
                            s.op("dve", lambda e, S=S, tm=tm, pat=pat: e.scalar_tensor_tensor(
                                out=tm[:, :], in0=S[:, 0:128], scalar=scale, in1=bias[:, pat, hh, :], op0=ALU.mult,
                                op1=ALU.add), (b_S, b_bias), (b_tm,))
                            s.op("act", lambda e, tm=tm, P=P: e.activation(out=P[:, 0:128], in_=tm[:, :],
                                                                           func=AF.Exp), (b_tm,), (b_P,))
                        s.op("pe", lambda e, P=P, kt=kt: e.matmul(O[:, 0:128], vv[:, kt, hh * 128:(hh + 1) * 128],
                                                                 P[:, 0:128], start=first, stop=last),
                             (b_vv, b_P), (b_O,), inc=False)
                        s.op("pe", lambda e, P=P: e.matmul(Dn[:, 0:128], ones_h[:, :], P[:, 0:128], start=first,
                                                          stop=last), (b_onesh, b_P), (b_D,))
                    finish(O, b_O, Dn, b_D, 128, 0, gb[:, hh * 128:(hh + 1) * 128], b_gb, brT[:, 1 * 2 + hh, bsl])
            s.sync_all()
        s.sync_all()
        c.stack = save
        s.stack = save


def fm(a):
    T, F = a.shape
    return np.ascontiguousarray(a.reshape(T, F // 128, 128).transpose(2, 1, 0))


def unfm(a):
    P, FC, T = a.shape
    return np.ascontiguousarray(a.transpose(2, 1, 0).reshape(T, FC * 128))


def wfm(w):
    K, N = w.shape
    return np.ascontiguousarray(w.reshape(K // 128, 128, N).transpose(1, 0, 2))


def vecfm(v):
    return np.ascontiguousarray(v.reshape(-1, 128).T)


def rope_tables(d_rot):
    t = np.arange(SEQ)
    row = (t // GRID_W).astype(np.float32)
    col = (t % GRID_W).astype(np.float32)
    n_f = d_rot // 4
    inv = np.power(np.float32(10000.0), -np.arange(n_f, dtype=np.float32) / np.float32(n_f)).astype(np.float32)
    ang = np.concatenate([row[:, None] * inv, col[:, None] * inv], axis=-1).astype(np.float32)
    half = d_rot // 2
    idx = np.arange(128) % half
    out = np.stack([np.cos(ang).astype(np.float32)[:, idx].T, np.sin(ang).astype(np.float32)[:, idx].T])
    return np.ascontiguousarray(out.astype(np.float32))


def rot_mats():
    RT = np.zeros((2, 128, 128), np.float32)
    for ri, d in enumerate((128, 64)):
        half = d // 2
        for m in range(128):
            if (m % d) < half:
                RT[ri, m + half, m] = -1.0
            else:
                RT[ri, m - half, m] = 1.0
    return RT


def tri_masks():
    kl = np.arange(128)[:, None]
    ql = np.arange(128)[None, :]
    prev = (kl >= ql).astype(np.float32)
    nxt = (kl <= ql).astype(np.float32)
    tri = np.stack([np.concatenate([prev, prev], 1), np.concatenate([nxt, nxt], 1)], axis=1)
    return np.ascontiguousarray(tri)


def l1_cols(r):
    h0 = 2 * r
    kv = r // 2
    cols = []
    cols += list(range(0 + h0 * 128, 0 + h0 * 128 + 256))
    cols += list(range(1024 + kv * 128, 1024 + kv * 128 + 128))
    cols += list(range(1280 + kv * 128, 1280 + kv * 128 + 128))
    cols += list(range(1536 + h0 * 128, 1536 + h0 * 128 + 256))
    for base in (2560, 3584, 4608, 5632):
        cols += list(range(base + h0 * 128, base + h0 * 128 + 256))
    cols += list(range(6656, 6656 + 896 + 512 + 64))
    cols += list(range(8128 + h0 * 128, 8128 + h0 * 128 + 256))
    for base in (9152, 10176, 11200, 12224):
        cols += list(range(base + h0 * 128, base + h0 * 128 + 256))
    assert len(cols) == W1COLS
    return np.asarray(cols)


def l1_inputs(inp, l, r, consts):
    w_in = inp["w_in"][l]
    h0 = 2 * r
    d = {}
    d["W1"] = wfm(w_in[:, l1_cols(r)])
    d["wqu"] = wfm(inp["w_q_up"][l][:, h0 * 192:(h0 + 2) * 192])
    d["wkvu"] = wfm(inp["w_kv_up"][l][:, h0 * 256:(h0 + 2) * 256])
    gv = np.zeros((128, NGV), np.float32)
    gv[:, G_QNA] = inp["qn_a"][l]
    gv[:, G_KNA] = inp["kn_a"][l]
    gv[:, G_QNB] = inp["qn_b"][l]
    gv[:, G_KNB] = inp["kn_b"][l]
    gv[:, G_QNN] = inp["qn_nope"][l]
    gv[:, G_KNN] = inp["kn_nope"][l]
    gv[:, G_QNP] = np.tile(inp["qn_pe"][l], 2)
    gv[:, G_KNP] = np.tile(inp["kn_pe"][l], 2)
    gv[:, G_QND] = np.tile(inp["qn_d"][l], 2)
    gv[:, G_KND] = np.tile(inp["kn_d"][l], 2)
    gv[:, G_SUB] = inp["subln_d"][l]
    gv[:, G_GQA:G_GQA + 7] = vecfm(inp["g_qa"][l])
    gv[:, G_GKVA:G_GKVA + 4] = vecfm(inp["g_kva"][l])
    gv[:, G_SINK] = inp["sink_a"][l][h0]
    gv[:, G_SINK + 1] = inp["sink_a"][l][h0 + 1]
    gv[0:64, G_LQ1] = inp["lam_q1"][l]
    gv[0:64, G_LK1] = inp["lam_k1"][l]
    gv[0:64, G_LQ2] = inp["lam_q2"][l]
    gv[0:64, G_LK2] = inp["lam_k2"][l]
    lambda_init = 0.8 - 0.6 * math.exp(-0.3 * l)
    gv[:, G_LINIT] = np.float32(lambda_init)
    gv[:, G_1MLI] = np.float32(1.0 - lambda_init)
    d["gvec"] = gv
    pats = consts["na_pats"]
    rpb = inp["rpb_b"][l]
    bias = np.full((128, len(pats), 2, 128), NEG, np.float32)
    for pi, (valid, dri, dci) in enumerate(pats):
        for hh in range(2):
            bias[:, pi, hh, :] = np.where(valid, rpb[h0 + hh][dri, dci], np.float32(NEG))
    d["biasB"] = bias
    d["ropeA"] = consts["ropeA"]
    d["ropeD"] = consts["ropeD"]
    d["RT"] = consts["RT"]
    d["tri"] = consts["tri"]
    return d


def make_consts():
    pairs, pats = na_patterns()
    return {"na_pats": pats, "ropeA": rope_tables(128), "ropeD": rope_tables(64), "RT": rot_mats(),
            "tri": tri_masks()}


_PROGS = {}


def _prog(name):
    if name not in _PROGS:
        _PROGS[name] = {"M": build_M, "N": build_N, "L1": build_L1, "L23": build_L23}[name]()
    return _PROGS[name]


def _run(name, in_maps):
    import time
    t0 = time.time()
    prog = _prog(name)
    t1 = time.time()
    res = run_bass_kernel_spmd(prog, in_maps, core_ids=list(range(NCORES)))
    print("[kernel] launch %s: build %.1fs run %.1fs" % (name, t1 - t0, time.time() - t1), flush=True)
    return res.results


def kernel(**inp):
    inp = {k: np.asarray(v) for k, v in inp.items()}
    x, cvec, ctx, c_ctx = inp["x"], inp["c"], inp["ctx"], inp["c_ctx"]
    consts = make_consts()

    call = np.zeros((4, D), np.float32)
    call[0], call[1], call[2] = cvec[0], cvec[1], c_ctx
    cT = np.ascontiguousarray(call.reshape(4, KC, 128).transpose(2, 1, 0))
    maps = []
    for j in range(NCORES):
        cols = slice(j * MCOLS, (j + 1) * MCOLS)
        wad = np.ascontiguousarray(inp["w_ada"][:, :, cols].reshape(DEPTH, KC, 128, MCOLS).transpose(0, 2, 1, 3))
        bad = np.ascontiguousarray(inp["b_ada"][:, cols].reshape(DEPTH, MCH, 128).transpose(2, 0, 1).reshape(128, DEPTH * MCH))
        maps.append({"cT": cT, "wad": wad, "bad": bad})
    outs = _run("M", maps)
    del maps
    mod = np.zeros((DEPTH, 4, 3 * D), np.float32)
    for j in range(NCORES):
        o = outs[j]["mod"].reshape(128, DEPTH, MCH, 4)
        mod[:, :, j * MCOLS:(j + 1) * MCOLS] = o.transpose(1, 3, 2, 0).reshape(DEPTH, 4, MCOLS)

    def modv(l, g):
        vs = [mod[l, g, 0:D], mod[l, g, D:2 * D], mod[l, g, 2 * D:3 * D],
              mod[l, 2, 0:D], mod[l, 2, D:2 * D], mod[l, 2, 2 * D:3 * D]]
        return np.ascontiguousarray(np.stack([vecfm(v) for v in vs], axis=1))

    xsh = []
    for core in range(NCORES):
        g, r = divmod(core, 4)
        xs = np.concatenate([x[g, TSH * r:TSH * (r + 1)], ctx[g, CSH * r:CSH * (r + 1)]], 0)
        xsh.append(fm(xs))

    outs = _run("N", [{"xT": xsh[core], "gn": vecfm(inp["g_norm"][0]), "modv": modv(0, core // 4)}
                      for core in range(NCORES)])
    hsh = [outs[core]["hT"] for core in range(NCORES)]

    for l in range(DEPTH):
        hfull = []
        for g in range(BATCH):
            parts = [hsh[4 * g + r][:, :, 0:TSH] for r in range(4)] + [hsh[4 * g + r][:, :, TSH:TS] for r in range(4)]
            hfull.append(np.ascontiguousarray(np.concatenate(parts, axis=2)))
        wl = [l1_inputs(inp, l, r, consts) for r in range(4)]
        maps = []
        for core in range(NCORES):
            g, r = divmod(core, 4)
            d = dict(wl[r])
            d["hT"] = hfull[g]
            maps.append(d)
        outs = _run("L1", maps)
        del maps, wl, hfull
        brsh = []
        for g in range(BATCH):
            br = np.empty((128, 32, TB), NPBF)
            for r in range(4):
                o = outs[4 * g + r]["brT"]
                for i in range(4):
                    br[:, i * 8 + 2 * r:i * 8 + 2 * r + 2, :] = o[:, i * 2:i * 2 + 2, :]
            for r in range(4):
                brsh.append(np.ascontiguousarray(np.concatenate(
                    [br[:, :, TSH * r:TSH * (r + 1)], br[:, :, SEQ + CSH * r:SEQ + CSH * (r + 1)]], axis=2)))
        del outs
        wmg = wfm(inp["w_in"][l][:, 13248:])
        wbr = np.ascontiguousarray(inp["w_branch"][l].reshape(4, 8, 128, D).transpose(2, 0, 1, 3).reshape(128, 32, D))
        wout = wfm(inp["w_out"][l])
        ln = min(l + 1, DEPTH - 1)
        gnn = vecfm(inp["g_norm"][ln])
        maps = [{"xT": xsh[core], "hT": hsh[core], "brT": brsh[core], "wmg": wmg, "wbr": wbr, "wout": wout,
                 "modv": modv(l, core // 4), "gn_next": gnn, "modv_next": modv(ln, core // 4)}
                for core in range(NCORES)]
        outs = _run("L23", maps)
        del maps, wmg, wbr, wout, brsh
        xsh = [outs[core]["xo"] for core in range(NCORES)]
        hsh = [outs[core]["ho"] for core in range(NCORES)]

    out = np.empty((BATCH, SEQ, D), np.float32)
    for core in range(NCORES):
        g, r = divmod(core, 4)
        out[g, TSH * r:TSH * (r + 1)] = unfm(xsh[core][:, :, 0:TSH])
    return out
```
